# Optimizing a Trainium2 kernel written in Bass

```python
import math
import jax, jax.numpy as jnp
from jax import lax
import numpy as np

D_MODEL = 2048
BATCH = 4
SEQ = 4096
DEPTH = 2

SB_HEADS = 8
SB_HEAD_DIM = 128
SB_WIDTH = SB_HEADS * SB_HEAD_DIM
POOL_WINDOWS = (2, 4, 8, 16)
POOL_GROUPS = len(POOL_WINDOWS)
POOL_WIDTH = D_MODEL - SB_WIDTH
POOL_GROUP_DIM = POOL_WIDTH // POOL_GROUPS
MIX_WIDTH = SB_WIDTH + POOL_WIDTH
EVEN_IN = 4 * SB_WIDTH + 2 * POOL_WIDTH
BLOCK_Q = 128

RWKV_HEAD_DIM = 64
RWKV_HEADS = D_MODEL // RWKV_HEAD_DIM
DECAY_LORA = 96
ICLR_LORA = 96
GN_EPS = 64e-5
L2_EPS = 1e-12

LN_EPS = 1e-5
DEEPNORM_ALPHA = (2 * DEPTH) ** 0.25
DEEPNORM_BETA = (8 * DEPTH) ** -0.25
N_EVEN = (DEPTH + 1) // 2
N_ODD = DEPTH // 2

kernel_name = "hybrid_stickbreak_pool_rwkv7"


def layer_norm(x, g, b):
    xf = x.astype(jnp.float32)
    mean = jnp.mean(xf, axis=-1, keepdims=True)
    var = jnp.mean(jnp.square(xf - mean), axis=-1, keepdims=True)
    return ((xf - mean) * lax.rsqrt(var + LN_EPS) * g + b).astype(x.dtype)


def stick_breaking_attention(q, k, v):
    S = q.shape[2]
    scale = SB_HEAD_DIM ** -0.5
    outs = []
    for start in range(0, S, BLOCK_Q):
        end = start + BLOCK_Q
        qb = q[:, :, start:end].astype(jnp.float32)
        kb = k[:, :, :end].astype(jnp.float32)
        vb = v[:, :, :end]
        z = jnp.einsum('bhqd,bhkd->bhqk', qb, kb) * scale
        t_idx = start + jnp.arange(BLOCK_Q)[:, None]
        s_idx = jnp.arange(end)[None, :]
        causal = s_idx < t_idx
        log_keep = jnp.where(causal, jax.nn.log_sigmoid(-z), 0.0)
        later = lax.cumsum(log_keep, axis=3, reverse=True) - log_keep
        weights = jnp.where(causal, jnp.exp(jax.nn.log_sigmoid(z) + later), 0.0)
        outs.append(jnp.einsum('bhqk,bhkd->bhqd', weights.astype(vb.dtype), vb))
    return jnp.concatenate(outs, axis=2)


def multiscale_pool(u, w_pool, pool_scale):
    B_, S, _ = u.shape
    ug = u.reshape(B_, S, POOL_GROUPS, POOL_GROUP_DIM).astype(jnp.float32)
    c0 = jnp.concatenate([jnp.zeros_like(ug[:, :1]), jnp.cumsum(ug, axis=1)], axis=1)
    pos = jnp.arange(1, S + 1, dtype=jnp.float32)
    outs = []
    for g, w in enumerate(POOL_WINDOWS):
        cg = c0[:, :, g]
        lower = jnp.concatenate([jnp.zeros_like(cg[:, :w - 1]), cg[:, :S - w + 1]], axis=1)
        count = jnp.minimum(pos, float(w))[None, :, None]
        outs.append((cg[:, 1:] - lower) / count - ug[:, :, g])
    pooled = jnp.stack(outs, axis=2)
    mixed = jnp.einsum('bsgc,gcd->bsgd', pooled, w_pool.astype(jnp.float32))
    return (mixed.reshape(B_, S, POOL_WIDTH) * pool_scale).astype(u.dtype)


def even_layer(x, w_in, w_pool, pool_scale, w_out):
    B_, S, _ = x.shape
    h = x @ w_in
    q, k, v, g_a, u, g_b = jnp.split(
        h, [SB_WIDTH, 2 * SB_WIDTH, 3 * SB_WIDTH, 4 * SB_WIDTH, 4 * SB_WIDTH + POOL_WIDTH], axis=-1)

    def heads(t):
        return t.reshape(B_, S, SB_HEADS, SB_HEAD_DIM).transpose(0, 2, 1, 3)

    o_a = stick_breaking_attention(heads(q), heads(k), heads(v))
    o_a = o_a.transpose(0, 2, 1, 3).reshape(B_, S, SB_WIDTH)
    o_b = multiscale_pool(u, w_pool, pool_scale)
    mixed = jnp.concatenate([o_a * jax.nn.silu(g_a), o_b * jax.nn.silu(g_b)], axis=-1)
    return mixed @ w_out


def wkv7_scan(r, w, k, v, kk, a):
    B_, S, H, N = r.shape

    def step(state, inp):
        r_t, w_t, k_t, v_t, kk_t, a_t = inp
        sa = jnp.einsum('bhvk,bhk->bhv', state, -kk_t)
        state = (state * w_t[:, :, None, :]
                 + sa[..., None] * (kk_t * a_t)[:, :, None, :]
                 + v_t[..., None] * k_t[:, :, None, :])
        return state, jnp.einsum('bhvk,bhk->bhv', state, r_t)

    xs = (jnp.moveaxis(r, 1, 0), jnp.moveaxis(w, 1, 0), jnp.moveaxis(k, 1, 0),
          jnp.moveaxis(v, 1, 0), jnp.moveaxis(kk, 1, 0), jnp.moveaxis(a, 1, 0))
    state0 = jnp.zeros((B_, H, N, N), jnp.float32)
    _, out = lax.scan(step, state0, xs)
    return jnp.moveaxis(out, 0, 1)


def odd_layer(x, mu, w_r, w_k, w_v, w_g, w0, w1, w2, a0, a1, a2, k_k, k_a, r_k, gn_w, gn_b, w_o):
    B_, S, D = x.shape
    H, N = RWKV_HEADS, RWKV_HEAD_DIM
    f32 = jnp.float32
    x_prev = jnp.pad(x, ((0, 0), (1, 0), (0, 0)))[:, :S]
    xx = x_prev - x
    xr = x + xx * mu[0]
    xw = x + xx * mu[1]
    xk = x + xx * mu[2]
    xv = x + xx * mu[3]
    xa = x + xx * mu[4]
    xg = x + xx * mu[5]
    r = (xr @ w_r).astype(f32)
    k = (xk @ w_k).astype(f32)
    v = (xv @ w_v).astype(f32)
    g = (xg @ w_g).astype(f32)
    w_log = -jax.nn.softplus(-(w0 + jnp.tanh(xw @ w1) @ w2).astype(f32)) - 0.5
    decay = jnp.exp(-jnp.exp(w_log))
    a = jax.nn.sigmoid((a0 + (xa @ a1) @ a2).astype(f32))
    kk = (k * k_k).reshape(B_, S, H, N)
    kk = kk / jnp.maximum(jnp.sqrt(jnp.sum(kk * kk, axis=-1, keepdims=True)), L2_EPS)
    k = k * (1.0 + (a - 1.0) * k_a)
    r = r.reshape(B_, S, H, N)
    k = k.reshape(B_, S, H, N)
    v = v.reshape(B_, S, H, N)
    o = wkv7_scan(r, decay.reshape(B_, S, H, N), k, v, kk, a.reshape(B_, S, H, N))
    mean = jnp.mean(o, axis=-1, keepdims=True)
    var = jnp.mean(jnp.square(o - mean), axis=-1, keepdims=True)
    o = ((o - mean) * lax.rsqrt(var + GN_EPS)).reshape(B_, S, D) * gn_w + gn_b
    bonus = jnp.sum(r * k * r_k, axis=-1, keepdims=True) * v
    o = (o + bonus.reshape(B_, S, D)) * jax.nn.silu(g)
    return o.astype(x.dtype) @ w_o


def setup_inputs(seed: int = 0) -> dict:
    key = jax.random.key(seed)
    ks = list(jax.random.split(key, 32))
    f32 = jnp.float32

    def nrm(i, shape, scale):
        return scale * jax.random.normal(ks[i], shape, f32)

    def unif(i, shape, lo, hi):
        return jax.random.uniform(ks[i], shape, f32, lo, hi)

    D = D_MODEL
    H, N = RWKV_HEADS, RWKV_HEAD_DIM
    return {
        'x': nrm(0, (BATCH, SEQ, D), 1.0),
        'ev_w_in': nrm(1, (N_EVEN, D, EVEN_IN), D ** -0.5),
        'ev_w_pool': nrm(2, (N_EVEN, POOL_GROUPS, POOL_GROUP_DIM, POOL_GROUP_DIM), POOL_GROUP_DIM ** -0.5),
        'ev_pool_scale': 1.0 + nrm(3, (N_EVEN, POOL_WIDTH), 0.1),
        'ev_w_out': nrm(4, (N_EVEN, MIX_WIDTH, D), DEEPNORM_BETA * MIX_WIDTH ** -0.5),
        'od_mu': unif(5, (N_ODD, 6, D), 0.0, 1.0),
        'od_w_r': nrm(6, (N_ODD, D, D), D ** -0.5),
        'od_w_k': nrm(7, (N_ODD, D, D), D ** -0.5),
        'od_w_v': nrm(8, (N_ODD, D, D), D ** -0.5),
        'od_w_g': nrm(9, (N_ODD, D, D), D ** -0.5),
        'od_w0': unif(10, (N_ODD, D), -6.0, 1.0),
        'od_w1': nrm(11, (N_ODD, D, DECAY_LORA), D ** -0.5),
        'od_w2': nrm(12, (N_ODD, DECAY_LORA, D), 0.5 * DECAY_LORA ** -0.5),
        'od_a0': nrm(13, (N_ODD, D), 0.1),
        'od_a1': nrm(14, (N_ODD, D, ICLR_LORA), D ** -0.5),
        'od_a2': nrm(15, (N_ODD, ICLR_LORA, D), 0.5 * ICLR_LORA ** -0.5),
        'od_k_k': 0.85 + nrm(16, (N_ODD, D), 0.05),
        'od_k_a': 1.0 + nrm(17, (N_ODD, D), 0.05),
        'od_r_k': nrm(18, (N_ODD, H, N), 0.1),
        'od_gn_w': 1.0 + nrm(19, (N_ODD, D), 0.05),
        'od_gn_b': nrm(20, (N_ODD, D), 0.02),
        'od_w_o': nrm(21, (N_ODD, D, D), DEEPNORM_BETA * D ** -0.5),
        'ln_g': 1.0 + nrm(22, (DEPTH, D), 0.02),
        'ln_b': nrm(23, (DEPTH, D), 0.02),
    }


def reference(x, ev_w_in, ev_w_pool, ev_pool_scale, ev_w_out,
              od_mu, od_w_r, od_w_k, od_w_v, od_w_g, od_w0, od_w1, od_w2,
              od_a0, od_a1, od_a2, od_k_k, od_k_a, od_r_k, od_gn_w, od_gn_b, od_w_o,
              ln_g, ln_b):
    for layer in range(DEPTH):
        j = layer // 2
        if layer % 2 == 0:
            y = even_layer(x, ev_w_in[j], ev_w_pool[j], ev_pool_scale[j], ev_w_out[j])
        else:
            y = odd_layer(x, od_mu[j], od_w_r[j], od_w_k[j], od_w_v[j], od_w_g[j],
                          od_w0[j], od_w1[j], od_w2[j], od_a0[j], od_a1[j], od_a2[j],
                          od_k_k[j], od_k_a[j], od_r_k[j], od_gn_w[j], od_gn_b[j], od_w_o[j])
        x = layer_norm(DEEPNORM_ALPHA * x + y, ln_g[layer], ln_b[layer])
    return x
```

```python
import numpy as np
import concourse.bass as bass
import concourse.mybir as mybir

F32 = mybir.dt.float32
BF16 = mybir.dt.bfloat16
AF = mybir.ActivationFunctionType
ALU = mybir.AluOpType
AX = mybir.AxisListType

SEM_SEG = 30000


class Res:
    __slots__ = ("name", "writer", "readers", "dsem", "dcount")

    def __init__(self, name):
        self.name = name
        self.writer = None
        self.readers = []
        self.dsem = None
        self.dcount = 0


class Tile:
    def __init__(self, h, name):
        self.h = h
        self.res = Res(name)

    def __getitem__(self, idx):
        return self.h[idx]


class View(Tile):
    def __init__(self, base, off, width, name):
        self.h = base.h
        self.off = off
        self.width = width
        self.res = base.res

    def __getitem__(self, idx):
        r, c = idx
        lo = 0 if c.start is None else c.start
        hi = self.width if c.stop is None else c.stop
        return self.h[r, self.off + lo:self.off + hi]


class Ins:
    __slots__ = ("eng", "fn", "deps", "is_dma", "sem", "val", "used", "key")

    def __init__(self, eng, fn, is_dma):
        self.eng = eng
        self.fn = fn
        self.deps = []
        self.is_dma = is_dma
        self.sem = None
        self.val = 0
        self.used = False
        self.key = None


def _res(x):
    return x.res if isinstance(x, Tile) else x


class KB:
    ENGS = ("pe", "act", "dve", "pool", "sp")

    def __init__(self):
        self.nc = bass.Bass("TRN2", target_bir_lowering=False)
        self.streams = {e: [] for e in self.ENGS}
        self.order = []
        self.n = 0
        self.sb_ptr = (self.nc.sbuf_base + 63) // 64 * 64
        self.sb_top = self.nc.sbuf_top
        self.pending_dmas = []
        self.uid = 0

    def dram(self, name, shape, dt, kind="Internal"):
        h = self.nc.dram_tensor(name, list(shape), dt, kind=kind)
        return Tile(h, name)

    def sb(self, name, shape, dt):
        n = 1
        for d in shape[1:]:
            n *= d
        nbytes = (n * mybir.dt.size(dt) + 63) // 64 * 64
        off = self.sb_ptr
        assert off + nbytes <= self.sb_top, f"SBUF overflow allocating {name}: {off + nbytes - self.sb_top} over"
        self.sb_ptr += nbytes
        self.uid += 1
        h = self.nc.alloc_sbuf_tensor_at(f"{name}_{self.uid}", list(shape), dt, offset=off)
        return Tile(h, name)

    def mark(self):
        return self.sb_ptr

    def release(self, m):
        self.barrier()
        self.sb_ptr = m

    def barrier(self):
        deps = []
        for e in self.ENGS:
            st = [i for i in self.streams[e] if i.fn is not None and not i.is_dma]
            if st:
                deps.append(st[-1])
        deps += self.pending_dmas
        self.pending_dmas = []
        for d in deps:
            d.used = True
        for e in self.ENGS:
            ins = Ins(e, None, False)
            ins.deps = list(deps)
            self.streams[e].append(ins)
            self.order.append(ins)

    def ps(self, name, shape, dt=F32):
        h = self.nc.alloc_psum_tensor(name, list(shape), dt)
        return Tile(h, name)

    def _record(self, eng, fn, reads, writes, is_dma=False, key=None):
        ins = Ins(eng, fn, is_dma)
        ins.key = key
        deps = []
        for r in reads:
            r = _res(r)
            w = r.writer
            if w is not None:
                deps.append(w)
            r.readers.append(ins)
        for wr in writes:
            wr = _res(wr)
            w = wr.writer
            if w is not None and (w.is_dma or is_dma or w.eng != eng or eng != 'pe'):
                deps.append(w)
            for rd in wr.readers:
                if rd is ins:
                    continue
                if rd.is_dma or is_dma or rd.eng != eng or eng != 'pe':
                    deps.append(rd)
            wr.writer = ins
            wr.readers = []
        seen = set()
        for d in deps:
            if id(d) not in seen and d is not ins:
                seen.add(id(d))
                d.used = True
                ins.deps.append(d)
        self.streams[eng].append(ins)
        self.order.append(ins)
        self.n += 1
        return ins

    def op(self, eng, fn, reads=(), writes=()):
        return self._record(eng, fn, reads, writes)

    def dma(self, q, out, in_, reads=(), writes=(), key=None, **kw):
        fn = lambda e, out=out, in_=in_, kw=kw: e.dma_start(out=out, in_=in_, **kw)
        ins = self._record(q, fn, reads, writes, is_dma=True, key=_res(key))
        ins.used = True
        self.pending_dmas.append(ins)
        return ins

    def mm(self, out, lhsT, rhs, start, stop, reads, writes, **kw):
        return self.op("pe", lambda e: e.matmul(out, lhsT, rhs, start=start, stop=stop, **kw),
                       reads, writes)

    def act(self, out, in_, func, reads, writes, eng="act", **kw):
        return self.op(eng, lambda e: e.activation(out=out, in_=in_, func=func, **kw), reads, writes)

    def finish(self, final_waits=()):
        nc = self.nc
        eng_sems = {}
        for ins in self.order:
            if ins.is_dma:
                r = ins.key
                if r.dsem is None or r.dcount + 16 > SEM_SEG:
                    r.dsem = nc.alloc_semaphore(f"d{len(eng_sems)}_{id(r) % 100000}_{r.dcount}")
                    r.dcount = 0
                    eng_sems[id(r.dsem)] = r.dsem
                r.dcount += 16
                ins.sem = r.dsem
                ins.val = r.dcount
        for e in self.ENGS:
            cnt = 0
            seg = None
            for ins in self.streams[e]:
                if ins.is_dma:
                    pass
                elif ins.used:
                    if seg is None or cnt + 1 > SEM_SEG:
                        seg = nc.alloc_semaphore(f"e_{e}_{len(eng_sems)}")
                        eng_sems[id(seg)] = seg
                        cnt = 0
                    cnt += 1
                    ins.sem = seg
                    ins.val = cnt
        engobj = {"pe": nc.tensor, "act": nc.scalar, "dve": nc.vector, "pool": nc.gpsimd, "sp": nc.sync}
        nwaits = [0]

        def replay(ename):
            def body(eng):
                known = {}
                for ins in self.streams[ename]:
                    for d in ins.deps:
                        k = id(d.sem)
                        if known.get(k, 0) >= d.val:
                            continue
                        known[k] = d.val
                        eng.wait_ge(d.sem, d.val)
                        nwaits[0] += 1
                    if ins.fn is None:
                        continue
                    bi = ins.fn(eng)
                    if ins.is_dma:
                        bi.then_inc(ins.sem, 16)
                    elif ins.used:
                        bi.then_inc(ins.sem, 1)
                if ename == "sp":
                    for d in final_waits:
                        eng.wait_ge(d.sem, d.val)
            return body

        with nc.Block() as block:
            block.tensor(replay("pe"))
            block.scalar(replay("act"))
            block.vector(replay("dve"))
            block.gpsimd(replay("pool"))
            block.sync(replay("sp"))
        self.nwaits = nwaits[0]
        return nc


D = 2048
ALPHA = float((2 * 2) ** 0.25)
LN_EPS = 1e-5


def build_outproj(NT=2048, emit_bf16=True):
    kb = KB()
    nc = kb.nc
    KT = D // 128
    mT = kb.dram("mT", [D, NT], BF16, kind="ExternalInput")
    w = kb.dram("w", [D, D], F32, kind="ExternalInput")
    x = kb.dram("x", [NT, D], F32, kind="ExternalInput")
    g = kb.dram("g", [1, D], F32, kind="ExternalInput")
    b = kb.dram("b", [1, D], F32, kind="ExternalInput")
    y = kb.dram("y", [NT, D], F32, kind="ExternalOutput")
    if emit_bf16:
        yb = kb.dram("yb", [NT, D], BF16, kind="ExternalOutput")

    wbf = [kb.sb(f"wbf{k}", [128, D], BF16) for k in range(KT)]
    G = kb.sb("G", [128, D], F32)
    Bt = kb.sb("Bt", [128, D], F32)
    TB = min(512, NT)
    mts = [[kb.sb(f"mt{i}_{k}", [128, TB], BF16) for k in range(KT)] for i in range(2)]
    xz = [kb.sb(f"xz{i}", [128, D], F32) for i in range(2)]
    ybt = [kb.sb(f"ybt{i}", [128, D], BF16) for i in range(2)]
    st = [kb.sb(f"st{i}", [128, 4, 6], F32) for i in range(2)]
    mv = [kb.sb(f"mv{i}", [128, 2], F32) for i in range(2)]
    rs = [kb.sb(f"rs{i}", [128, 1], F32) for i in range(2)]
    nmr = [kb.sb(f"nmr{i}", [128, 1], F32) for i in range(2)]
    pss = [[kb.ps(f"ps{i}_{c}", [128, 512], F32) for c in range(4)] for i in range(2)]

    wv = w.h.ap().rearrange("(k p) n -> k p n", p=128)
    for k in range(KT):
        kb.dma("pool", wbf[k][:, :], wv[k], writes=[wbf[k]], key=wbf[k])
    kb.dma("sp", G[:, :], bass.AP(g.h, 0, [[0, 128], [1, D]]), writes=[G], key=G)
    kb.dma("sp", Bt[:, :], bass.AP(b.h, 0, [[0, 128], [1, D]]), writes=[Bt], key=Bt)

    mTv = mT.h.ap().rearrange("(k p) n -> k p n", p=128)
    last = []
    ntt = NT // 128
    for tt in range(ntt):
        i = tt % 2
        tb = tt // (TB // 128)
        j = tb % 2
        if tt % (TB // 128) == 0:
            for k in range(KT):
                kb.dma("sp", mts[j][k][:, :], mTv[k][:, tb * TB:(tb + 1) * TB], writes=[mts[j][k]], key=mts[j][k])
        toff = (tt % (TB // 128)) * 128
        kb.dma("act", xz[i][:, :], x.h.ap()[tt * 128:(tt + 1) * 128, :], writes=[xz[i]], key=xz[i])
        for c in range(4):
            for k in range(KT):
                kb.mm(pss[i][c][:, :], mts[j][k][:, toff:toff + 128], wbf[k][:, c * 512:(c + 1) * 512],
                      start=(k == 0), stop=(k == KT - 1), reads=[mts[j][k], wbf[k]], writes=[pss[i][c]])
        for c in range(4):
            sl = slice(c * 512, (c + 1) * 512)
            kb.op("dve", lambda e, i=i, c=c, sl=sl: e.scalar_tensor_tensor(
                out=xz[i][:, sl], in0=xz[i][:, sl], scalar=ALPHA, in1=pss[i][c][:, :],
                op0=ALU.mult, op1=ALU.add), reads=[xz[i], pss[i][c]], writes=[xz[i]])
        for c in range(4):
            kb.op("dve", lambda e, i=i, c=c: e.bn_stats(out=st[i][:, c, :], in_=xz[i][:, c * 512:(c + 1) * 512]),
                  reads=[xz[i]], writes=[st[i]])
        kb.op("dve", lambda e, i=i: e.bn_aggr(out=mv[i][:, :], in_=st[i][:, :, :].rearrange("p c s -> p (c s)")),
              reads=[st[i]], writes=[mv[i]])
        kb.op("dve", lambda e, i=i: e.tensor_scalar_add(out=rs[i][:, :], in0=mv[i][:, 1:2], scalar1=LN_EPS),
              reads=[mv[i]], writes=[rs[i]])
        kb.op("act", lambda e, i=i: e.sqrt(out=rs[i][:, :], in_=rs[i][:, :]), reads=[rs[i]], writes=[rs[i]])
        kb.op("dve", lambda e, i=i: e.reciprocal(out=rs[i][:, :], in_=rs[i][:, :]), reads=[rs[i]], writes=[rs[i]])
        kb.op("dve", lambda e, i=i: e.scalar_tensor_tensor(out=nmr[i][:, :], in0=mv[i][:, 0:1], scalar=-1.0, in1=rs[i][:, :],
                                                           op0=ALU.mult, op1=ALU.mult), reads=[mv[i], rs[i]], writes=[nmr[i]])
        kb.act(xz[i][:, :], xz[i][:, :], AF.Identity, reads=[xz[i], rs[i], nmr[i]], writes=[xz[i]],
               scale=rs[i][:, 0:1], bias=nmr[i][:, 0:1])
        kb.op("pool", lambda e, i=i: e.tensor_tensor(out=xz[i][:, :], in0=xz[i][:, :], in1=G[:, :], op=ALU.mult),
              reads=[xz[i], G], writes=[xz[i]])
        kb.op("dve", lambda e, i=i: e.tensor_tensor(out=xz[i][:, :], in0=xz[i][:, :], in1=Bt[:, :], op=ALU.add),
              reads=[xz[i], Bt], writes=[xz[i]])
        d1 = kb.dma("sp", y.h.ap()[tt * 128:(tt + 1) * 128, :], xz[i][:, :], reads=[xz[i]], key=xz[i])
        last.append(d1)
        if emit_bf16:
            kb.op("act", lambda e, i=i: e.copy(out=ybt[i][:, :], in_=xz[i][:, :]), reads=[xz[i]], writes=[ybt[i]])
            d2 = kb.dma("sp", yb.h.ap()[tt * 128:(tt + 1) * 128, :], ybt[i][:, :], reads=[ybt[i]], key=ybt[i])
            last.append(d2)
    kb.finish(final_waits=last[-4:])
    return kb


T_SEQ = 4096
SB_SCALE = 128 ** -0.5


def build_mixer0(T=T_SEQ):
    kb = KB()
    KT = D // 128
    NTB = T // 512
    NCT = 24
    xT = kb.dram("xT", [D, T], F32, kind="ExternalInput")
    wA = kb.dram("wA", [D, 3072], F32, kind="ExternalInput")
    wp = kb.dram("wp", [2, 256, 256], F32, kind="ExternalInput")
    psc = kb.dram("psc", [128, 4], F32, kind="ExternalInput")
    cbf = kb.dram("cbf", [128, 512], BF16, kind="ExternalInput")
    cf = kb.dram("cf", [128, 128], F32, kind="ExternalInput")
    pcoef = kb.dram("pcoef", [4, 128, 4], F32, kind="ExternalInput")
    ptab = kb.dram("ptab", [4, 128, 64], F32, kind="ExternalInput")
    mixT = kb.dram("mixT", [1024, T], BF16, kind="ExternalOutput")
    qk_s = kb.dram("qk_s", [8, 128, T], BF16)
    sg_s = kb.dram("sg_s", [8, 128, T], BF16)
    u_s = kb.dram("u_s", [4, 128, T], F32)
    v_s = kb.dram("v_s", [T, 512], BF16)

    banks = [kb.ps(f"bank{i}", [128, 512], F32) for i in range(8)]
    cb = kb.sb("cb", [128, 512], BF16)
    cm = kb.sb("cm", [128, 128], F32)
    kb.dma("sp", cb[:, :], cbf.h.ap(), writes=[cb], key=cb)
    kb.dma("sp", cm[:, :], cf.h.ap(), writes=[cm], key=cm)
    TRI = cb[:, 0:128]
    LOW = cb[:, 128:256]
    ZERO = cb[:, 256:384]
    MASKB = cb[:, 384:512]

    m0 = kb.mark()
    wbf = [kb.sb(f"wbf{k}", [128, 3072], BF16) for k in range(KT)]
    wv = wA.h.ap().rearrange("(k p) n -> k p n", p=128)
    for k in range(KT):
        kb.dma("pool", wbf[k][:, :], wv[k], writes=[wbf[k]], key=wbf[k])
    xb = [kb.sb(f"xb{i}", [128, KT, 512], BF16) for i in range(2)]
    stb = [kb.sb(f"stb{i}", [128, 512], BF16) for i in range(4)]
    stf = [kb.sb(f"stf{i}", [128, 512], F32) for i in range(2)]
    xTv = xT.h.ap().rearrange("(k p) n -> p k n", p=128)
    nb = 0
    nf = 0
    ne = 0
    for tb in range(NTB):
        j = tb % 2
        tsl = slice(tb * 512, (tb + 1) * 512)
        kb.dma("pool", xb[j][:, :, :], xTv[:, :, tsl], writes=[xb[j]], key=xb[j])
        for ct in range(NCT):
            kind = ct // 4
            if kind == 2:
                continue
            bank = banks[ne % 4]
            for k in range(KT):
                kb.mm(bank[:, :], wbf[k][:, ct * 128:(ct + 1) * 128], xb[j][:, k, :],
                      start=(k == 0), stop=(k == KT - 1), reads=[wbf[k], xb[j]], writes=[bank])
            if kind == 4:
                st = stf[nf % 2]
                nf += 1
                kb.op("dve", lambda e, st=st, bank=bank: e.tensor_copy(out=st[:, :], in_=bank[:, :]),
                      reads=[bank], writes=[st])
                dst = u_s.h.ap()[ct - 16, :, tsl]
            else:
                st = stb[nb % 4]
                nb += 1
                if kind in (3, 5):
                    kb.act(st[:, :], bank[:, :], AF.Silu, reads=[bank], writes=[st])
                    dst = sg_s.h.ap()[(ct - 12) if kind == 3 else (ct - 20 + 4), :, tsl]
                else:
                    if ne % 2 == 0:
                        kb.op("dve", lambda e, st=st, bank=bank: e.tensor_copy(out=st[:, :], in_=bank[:, :]),
                              reads=[bank], writes=[st])
                    else:
                        kb.op("act", lambda e, st=st, bank=bank: e.copy(out=st[:, :], in_=bank[:, :]),
                              reads=[bank], writes=[st])
                    dst = qk_s.h.ap()[ct, :, tsl]
            kb.dma("sp", dst, st[:, :], reads=[st], key=st)
            ne += 1
        for sub in range(4):
            bank = banks[ne % 4]
            for k in range(KT):
                kb.mm(bank[:, :], xb[j][:, k, sub * 128:(sub + 1) * 128], wbf[k][:, 1024:1536],
                      start=(k == 0), stop=(k == KT - 1), reads=[wbf[k], xb[j]], writes=[bank])
            st = stb[nb % 4]
            nb += 1
            kb.op("dve", lambda e, st=st, bank=bank: e.tensor_copy(out=st[:, :], in_=bank[:, :]),
                  reads=[bank], writes=[st])
            r0 = tb * 512 + sub * 128
            kb.dma("sp", v_s.h.ap()[r0:r0 + 128, :], st[:, :], reads=[st], key=st)
            ne += 1
    kb.release(m0)

    m1 = kb.mark()
    qT = [kb.sb(f"qT{i}", [128, T], BF16) for i in range(2)]
    kT = [kb.sb(f"kT{i}", [128, T], BF16) for i in range(2)]
    gT = [kb.sb(f"gT{i}", [128, T], BF16) for i in range(2)]
    vT = [kb.sb(f"vT{i}", [128, T // 128, 128], BF16) for i in range(2)]
    Et = [kb.sb(f"E{i}", [128, 512], F32) for i in range(3)]
    Xt = [kb.sb(f"X{i}", [128, 512], F32) for i in range(3)]
    SPt = [kb.sb(f"SP{i}", [128, 512], BF16) for i in range(3)]
    Wt = [kb.sb(f"W{i}", [128, 512], BF16) for i in range(3)]
    ost = [kb.sb(f"ost{i}", [128, 512], BF16) for i in range(2)]
    Zb = [banks[0], banks[1]]
    Accb = [banks[2], banks[3]]
    Ob = [banks[4], banks[5]]
    gcount = 0
    for h in range(4):
        hb = h % 2
        kb.dma("sp", qT[hb][:, :], qk_s.h.ap()[h], writes=[qT[hb]], key=qT[hb])
        kb.dma("sp", kT[hb][:, :], qk_s.h.ap()[4 + h], writes=[kT[hb]], key=kT[hb])
        kb.dma("act", gT[hb][:, :], sg_s.h.ap()[h], writes=[gT[hb]], key=gT[hb])
        kb.dma("act", vT[hb][:, :, :], v_s.h.ap()[:, h * 128:(h + 1) * 128].rearrange("(n p) d -> p n d", p=128),
               writes=[vT[hb]], key=vT[hb])
        for qg in range(T // 512):
            Acc = Accb[gcount % 2]
            O = Ob[gcount % 2]
            gcount += 1
            q0 = qg * 512
            kb.mm(Acc[:, :], ZERO, qT[hb][:, 0:512], start=True, stop=True, reads=[cb, qT[hb]], writes=[Acc])
            kb.mm(O[:, :], ZERO, qT[hb][:, 0:512], start=True, stop=True, reads=[cb, qT[hb]], writes=[O])
            kbs = list(range(4 * qg + 3, -1, -1))
            n = len(kbs)
            c0s = [max(0, kk - 4 * qg) * 128 for kk in kbs]
            for it in range(n + 2):
                if it < n:
                    idx = it
                    kk = kbs[idx]
                    c0 = c0s[idx]
                    Z = Zb[idx % 2]
                    E = Et[idx % 3]
                    SP = SPt[idx % 3]
                    kb.mm(Z[:, c0:], kT[hb][:, kk * 128:(kk + 1) * 128], qT[hb][:, q0 + c0:q0 + 512],
                          start=True, stop=True, reads=[kT[hb], qT[hb]], writes=[Z])
                    kb.act(E[:, c0:], Z[:, c0:], AF.Exp, reads=[Z], writes=[E], scale=SB_SCALE)
                    kb.act(SP[:, c0:], E[:, c0:], AF.Ln, reads=[E], writes=[SP], bias=1.0)
                    if kk >= 4 * qg:
                        kb.op("pool", lambda e, SP=SP, c0=c0: e.tensor_tensor(
                            out=SP[:, c0:c0 + 128], in0=SP[:, c0:c0 + 128], in1=MASKB, op=ALU.mult),
                            reads=[SP, cb], writes=[SP])
                        kb.op("pool", lambda e, E=E, c0=c0: e.tensor_tensor(
                            out=E[:, c0:c0 + 128], in0=E[:, c0:c0 + 128], in1=cm[:, :], op=ALU.mult),
                            reads=[E, cm], writes=[E])
                if 1 <= it <= n:
                    idx = it - 1
                    c0 = c0s[idx]
                    E = Et[idx % 3]
                    SP = SPt[idx % 3]
                    X = Xt[idx % 3]
                    W = Wt[idx % 3]
                    kb.mm(Acc[:, c0:], TRI, SP[:, c0:], start=False, stop=True, reads=[cb, SP], writes=[Acc],
                          skip_group_check=True)
                    if idx >= 1:
                        pc0 = c0s[idx - 1]
                        PSP = SPt[(idx - 1) % 3]
                        kb.mm(Acc[:, pc0:], LOW, PSP[:, pc0:], start=False, stop=True, reads=[cb, PSP], writes=[Acc],
                              skip_group_check=True)
                    kb.act(X[:, c0:], Acc[:, c0:], AF.Exp, reads=[Acc], writes=[X], scale=-1.0)
                    kb.op("dve", lambda e, W=W, E=E, X=X, c0=c0: e.tensor_tensor(
                        out=W[:, c0:], in0=E[:, c0:], in1=X[:, c0:], op=ALU.mult), reads=[E, X], writes=[W])
                if it >= 2:
                    idx = it - 2
                    kk = kbs[idx]
                    c0 = c0s[idx]
                    W = Wt[idx % 3]
                    kb.mm(O[:, c0:], vT[hb][:, kk, :], W[:, c0:], start=False, stop=True, reads=[vT[hb], W], writes=[O],
                          skip_group_check=True)
            os_ = ost[qg % 2]
            kb.op("dve", lambda e, os_=os_, O=O, hb=hb, q0=q0: e.tensor_tensor(
                out=os_[:, :], in0=O[:, :], in1=gT[hb][:, q0:q0 + 512], op=ALU.mult), reads=[O, gT[hb]], writes=[os_])
            kb.dma("sp", mixT.h.ap()[h * 128:(h + 1) * 128, q0:q0 + 512], os_[:, :], reads=[os_], key=os_)
    kb.release(m1)

    PADL = 16
    TP = T + PADL
    wpb = kb.sb("wpb", [128, 2, 2, 256], BF16)
    kb.dma("pool", wpb[:, :, :, :], wp.h.ap().rearrange("g (ci p) d -> p g ci d", p=128), writes=[wpb], key=wpb)
    pscT = kb.sb("pscT", [128, 4], F32)
    kb.dma("sp", pscT[:, :], psc.h.ap(), writes=[pscT], key=pscT)
    coef = kb.sb("coef", [128, 4, 4], F32)
    kb.dma("sp", coef[:, :, :], pcoef.h.ap().rearrange("c p k -> p c k"), writes=[coef], key=coef)
    tab = kb.sb("tab", [128, 4, 64], F32)
    kb.dma("sp", tab[:, :, :], ptab.h.ap().rearrange("c p k -> p c k"), writes=[tab], key=tab)
    S = [kb.sb(f"S{i}", [128, TP], F32) for i in range(5)]
    accs = kb.sb("acc", [128, T], F32)
    t16 = kb.sb("t16", [128, 16], F32)
    pooled = [kb.sb(f"pooled{i}", [128, T], BF16) for i in range(2)]
    sgB = [kb.sb(f"sgB{i}", [128, T], BF16) for i in range(2)]
    pst = [kb.sb(f"pst{i}", [128, 512], BF16) for i in range(2)]
    for i in range(5):
        kb.op("pool", lambda e, i=i: e.memset(S[i][:, 0:PADL], 0.0), writes=[S[i]])
    npst = 0
    for g in range(2):
        for ci in range(2):
            c = 2 * g + ci
            kb.dma("sp", S[0][:, PADL:], u_s.h.ap()[c], writes=[S[0]], key=S[0])
            for lv in range(4):
                sh = 1 << lv
                eng = "dve" if lv % 2 == 0 else "pool"
                kb.op(eng, lambda e, lv=lv, sh=sh: e.tensor_tensor(
                    out=S[lv + 1][:, PADL:], in0=S[lv][:, PADL:], in1=S[lv][:, PADL - sh:TP - sh], op=ALU.add),
                    reads=[S[lv]], writes=[S[lv + 1]])
            kb.op("dve", lambda e, c=c: e.scalar_tensor_tensor(
                out=accs[:, :], in0=S[1][:, PADL:], scalar=coef[:, c, 0:1], in1=S[0][:, PADL:],
                op0=ALU.mult, op1=ALU.subtract), reads=[S[1], S[0], coef], writes=[accs])
            for lv in range(1, 4):
                eng = "dve"
                kb.op(eng, lambda e, c=c, lv=lv: e.scalar_tensor_tensor(
                    out=accs[:, :], in0=S[lv + 1][:, PADL:], scalar=coef[:, c, lv:lv + 1], in1=accs[:, :],
                    op0=ALU.mult, op1=ALU.add), reads=[S[lv + 1], accs, coef], writes=[accs])
            kb.op("dve", lambda e, c=c: e.tensor_tensor(
                out=t16[:, :], in0=S[1][:, PADL:PADL + 16], in1=tab[:, c, 0:16], op=ALU.mult),
                reads=[S[1], tab], writes=[t16])
            kb.op("dve", lambda e: e.tensor_tensor(
                out=accs[:, 0:16], in0=t16[:, :], in1=S[0][:, PADL:PADL + 16], op=ALU.subtract),
                reads=[t16, S[0], accs], writes=[accs])
            for lv in range(1, 4):
                kb.op("dve", lambda e, c=c, lv=lv: e.tensor_tensor(
                    out=t16[:, :], in0=S[lv + 1][:, PADL:PADL + 16], in1=tab[:, c, lv * 16:(lv + 1) * 16], op=ALU.mult),
                    reads=[S[lv + 1], tab], writes=[t16])
                kb.op("dve", lambda e: e.tensor_tensor(
                    out=accs[:, 0:16], in0=accs[:, 0:16], in1=t16[:, :], op=ALU.add),
                    reads=[t16, accs], writes=[accs])
            kb.op("act", lambda e, ci=ci: e.copy(out=pooled[ci][:, :], in_=accs[:, :]), reads=[accs], writes=[pooled[ci]])
        for dt_ in range(2):
            c = 2 * g + dt_
            kb.dma("act", sgB[dt_][:, :], sg_s.h.ap()[4 + c], writes=[sgB[dt_]], key=sgB[dt_])
            for tb in range(NTB):
                tsl = slice(tb * 512, (tb + 1) * 512)
                bank = banks[6 + (tb % 2)]
                for ci in range(2):
                    kb.mm(bank[:, :], wpb[:, g, ci, dt_ * 128:(dt_ + 1) * 128], pooled[ci][:, tsl],
                          start=(ci == 0), stop=(ci == 1), reads=[wpb, pooled[ci]], writes=[bank])
                st = pst[npst % 2]
                npst += 1
                kb.op("dve", lambda e, st=st, bank=bank, c=c, dt_=dt_, tsl=tsl: e.scalar_tensor_tensor(
                    out=st[:, :], in0=bank[:, :], scalar=pscT[:, c:c + 1], in1=sgB[dt_][:, tsl],
                    op0=ALU.mult, op1=ALU.mult), reads=[bank, pscT, sgB[dt_]], writes=[st])
                kb.dma("sp", mixT.h.ap()[512 + c * 128:512 + (c + 1) * 128, tsl], st[:, :], reads=[st], key=st)
    kb.barrier()
    kb.finish()
    return kb


POOL_WINDOWS = (2, 4, 8, 16)


def host_consts():
    import ml_dtypes
    j = np.arange(128)[:, None]
    s = np.arange(128)[None, :]
    tri = (j >= s).astype(np.float32)
    low = (j < s).astype(np.float32)
    zero = np.zeros((128, 128), np.float32)
    mask = (j < s).astype(np.float32)
    cbf = np.concatenate([tri, low, zero, mask], axis=1).astype(ml_dtypes.bfloat16)
    return cbf, mask


def prep_mixer0(xb, w_in, w_pool, pool_scale, hh):
    xT = np.ascontiguousarray(xb.T)
    sl = slice(hh * 512, (hh + 1) * 512)
    parts = [w_in[:, 0 * 1024:1 * 1024][:, sl], w_in[:, 1024:2048][:, sl], w_in[:, 2048:3072][:, sl],
             w_in[:, 3072:4096][:, sl], w_in[:, 4096:5120][:, sl], w_in[:, 5120:6144][:, sl]]
    wA = np.ascontiguousarray(np.concatenate(parts, axis=1))
    wp = np.ascontiguousarray(w_pool[2 * hh:2 * hh + 2])
    psc = np.ascontiguousarray(pool_scale[sl].reshape(4, 128).T)
    pcoef = np.zeros((4, 128, 4), np.float32)
    ptab = np.zeros((4, 128, 64), np.float32)
    for c in range(4):
        w = POOL_WINDOWS[2 * hh + c // 2]
        for lv in range(4):
            if (2 << lv) == w:
                pcoef[c, :, lv] = 1.0 / w
                for t in range(16):
                    ptab[c, :, lv * 16 + t] = 1.0 / min(t + 1, w)
    cbf, mask = host_consts()
    return dict(xT=xT, wA=wA, wp=wp, psc=psc, cbf=cbf, cf=mask, pcoef=pcoef, ptab=ptab)


CH = 64
GN_EPS = 64e-5
DEC_C = float(np.exp(-0.5))


def sb_ap(tile_, part0, nparts, offset, dims):
    h = tile_.h
    F = 1
    for d in h.shape[1:]:
        F *= d
    return bass.AP(h, part0 * F + offset, [[F, nparts]] + [list(d) for d in dims])


def build_rwkv(T=T_SEQ, BT=256, stop_after=None, upto=99):
    kb = KB()
    NCK = BT // CH
    NB2 = T // BT
    KT = D // 128
    NTB = T // 512
    x1T = kb.dram("x1T", [D, T], BF16, kind="ExternalInput")
    wbig = [kb.dram(n, [D, 1024], F32, kind="ExternalInput") for n in ("wr", "wk", "wv", "wg")]
    w1 = kb.dram("w1", [D, 96], F32, kind="ExternalInput")
    a1 = kb.dram("a1", [D, 96], F32, kind="ExternalInput")
    w2 = kb.dram("w2", [96, 1024], F32, kind="ExternalInput")
    a2 = kb.dram("a2", [96, 1024], F32, kind="ExternalInput")
    mu6 = kb.dram("mu6", [128, 6, KT], F32, kind="ExternalInput")
    vecs = kb.dram("vecs", [128, 8, 8], F32, kind="ExternalInput")
    cbf = kb.dram("ccb", [128, 256], BF16, kind="ExternalInput")
    cf = kb.dram("ccf", [128, 320], F32, kind="ExternalInput")
    rmk = kb.dram("rmk", [128, BT], F32, kind="ExternalInput")
    oT = kb.dram("oT", [1024, T], BF16, kind="ExternalOutput")
    rk_s = kb.dram("rk_s", [2, 8, 128, T], F32)
    v_s = kb.dram("v_s", [8, 128, T], BF16)
    sg_s = kb.dram("sg_s", [8, 128, T], BF16)

    bM = [kb.ps(f"bM{i}", [128, 512], F32) for i in range(2)]
    bOS = kb.ps("bOS", [128, 512], F32)
    pbk = [kb.ps("pbk0", [128, 512], F32)] * 2
    bkA = kb.ps("bkA", [128, 512], F32)
    bkB = kb.ps("bkB", [128, 512], F32)
    bkC = kb.ps("bkC", [128, 512], F32)
    bP = View(bkA, 0, 256, "bP")
    bTU = View(bkB, 0, 256, "bTU")
    bU = View(bkB, 256, 256, "bU")
    bX = View(bkC, 0, 256, "bX")
    bT = View(bkC, 256, 256, "bT")
    bTRb = kb.ps("bTRb", [128, 1024], BF16)
    banks = [bM[0], bM[1], bOS, bkC]
    cb = kb.sb("cb", [128, 256], BF16)
    cfm = kb.sb("cfm", [128, 320], F32)
    rmask = kb.sb("rmask", [128, BT], F32)
    mus = kb.sb("mus", [128, 6, KT], F32)
    vec = kb.sb("vec", [128, 8, 8], F32)
    kb.dma("sp", cb[:, :], cbf.h.ap(), writes=[cb], key=cb)
    kb.dma("sp", cfm[:, :], cf.h.ap(), writes=[cfm], key=cfm)
    kb.dma("sp", rmask[:, :], rmk.h.ap(), writes=[rmask], key=rmask)
    kb.dma("sp", mus[:, :, :], mu6.h.ap(), writes=[mus], key=mus)
    kb.dma("sp", vec[:, :, :], vecs.h.ap(), writes=[vec], key=vec)
    IDENT = cb[:, 0:128]
    BONES = cb[:, 128:256]
    MALL = cfm[:, 0:128]
    MASKT = cfm[0:64, 128:192]
    ID64 = cfm[0:64, 192:256]
    tanhT = kb.sb("tanhT", [96, T], BF16)
    aT = kb.sb("aT", [96, T], BF16)
    w2b = kb.sb("w2b", [96, 1024], BF16)
    a2b = kb.sb("a2b", [96, 1024], BF16)
    kb.dma("pool", w2b[:, :], w2.h.ap(), writes=[w2b], key=w2b)
    kb.dma("pool", a2b[:, :], a2.h.ap(), writes=[a2b], key=a2b)

    m0 = kb.mark()
    wsb = [kb.sb(f"wsb{i}", [128, KT, 1024], BF16) for i in range(2)]
    w1b = kb.sb("w1b", [128, KT, 96], BF16)
    a1b = kb.sb("a1b", [128, KT, 96], BF16)
    xb = [kb.sb(f"xb{i}", [128, KT, 513], BF16) for i in range(2)]
    xm = [kb.sb(f"xm{i}", [128, KT, 512], BF16) for i in range(2)]
    xm2 = kb.sb("xm2", [128, KT, 512], BF16)
    dtmp = [kb.sb(f"dtmp{i}", [128, 512], F32) for i in range(4)]
    dtmp2 = [kb.sb(f"dtmp2{i}", [128, 512], F32) for i in range(2)]
    stf = [kb.sb(f"stf{i}", [128, 512], F32) for i in range(3)]
    stb = [kb.sb(f"stb{i}", [128, 512], BF16) for i in range(3)]
    x1v = x1T.h.ap().rearrange("(k p) n -> p k n", p=128)
    kb.dma("pool", w1b[:, :, :], w1.h.ap().rearrange("(k p) n -> p k n", p=128), writes=[w1b], key=w1b)
    kb.dma("pool", a1b[:, :, :], a1.h.ap().rearrange("(k p) n -> p k n", p=128), writes=[a1b], key=a1b)
    MU_IDX = {"r": 0, "w": 1, "k": 2, "v": 3, "a": 4, "g": 5}
    cnt = {"x": 0, "d": 0, "d2": 0, "f": 0, "b": 0, "e": 0, "m": 0}

    def load_x(tb):
        j = cnt["x"] % 2
        cnt["x"] += 1
        X = xb[j]
        if tb == 0:
            kb.op("pool", lambda e, X=X: e.memset(X[:, :, 0:1], 0.0), writes=[X])
            kb.dma("sp", X[:, :, 1:513], x1v[:, :, 0:512], writes=[X], key=X)
        else:
            kb.dma("sp", X[:, :, :], x1v[:, :, tb * 512 - 1:tb * 512 + 512], writes=[X], key=X)
        return X

    def make_mix(X, mixes):
        for k in range(KT):
            dt_ = dtmp[cnt["d"] % 4]
            cnt["d"] += 1
            eng = "dve" if k % 2 == 0 else "pool"
            kb.op(eng, lambda e, dt_=dt_, X=X, k=k: e.tensor_tensor(
                out=dt_[:, :], in0=X[:, k, 0:512], in1=X[:, k, 1:513], op=ALU.subtract), reads=[X], writes=[dt_])
            for (mn, out_t) in mixes:
                mi = MU_IDX[mn]
                if eng == "dve":
                    kb.op(eng, lambda e, dt_=dt_, X=X, k=k, mi=mi, out_t=out_t: e.scalar_tensor_tensor(
                        out=out_t[:, k, :], in0=dt_[:, :], scalar=mus[:, mi, k:k + 1], in1=X[:, k, 1:513],
                        op0=ALU.mult, op1=ALU.add), reads=[dt_, X, mus], writes=[out_t])
                else:
                    d2 = dtmp2[cnt["d2"] % 2]
                    cnt["d2"] += 1
                    kb.op(eng, lambda e, dt_=dt_, d2=d2, k=k, mi=mi: e.tensor_scalar(
                        out=d2[:, :], in0=dt_[:, :], scalar1=mus[:, mi, k:k + 1], scalar2=None, op0=ALU.mult),
                        reads=[dt_, mus], writes=[d2])
                    kb.op(eng, lambda e, d2=d2, X=X, k=k, out_t=out_t: e.tensor_tensor(
                        out=out_t[:, k, :], in0=d2[:, :], in1=X[:, k, 1:513], op=ALU.add), reads=[d2, X], writes=[out_t])

    for pi, mn in enumerate(("r", "k", "v", "g")):
        W = wsb[pi % 2]
        wv_ = wbig[pi].h.ap().rearrange("(k p) n -> p k n", p=128)
        for k in range(KT):
            kb.dma("pool", W[:, k, :], wv_[:, k, :], writes=[W], key=W)
        for tb in range(NTB):
            tsl = slice(tb * 512, (tb + 1) * 512)
            X = load_x(tb)
            XM = xm[cnt["m"] % 2]
            cnt["m"] += 1
            make_mix(X, [(mn, XM)])
            for ct in range(8):
                bank = banks[cnt["e"] % 4]
                cnt["e"] += 1
                for k in range(KT):
                    kb.mm(bank[:, :], W[:, k, ct * 128:(ct + 1) * 128], XM[:, k, :],
                          start=(k == 0), stop=(k == KT - 1), reads=[W, XM], writes=[bank])
                if mn in ("r", "k"):
                    st = stf[cnt["f"] % 3]
                    cnt["f"] += 1
                    if ct % 2 == 0:
                        kb.op("act", lambda e, st=st, bank=bank: e.copy(out=st[:, :], in_=bank[:, :]), reads=[bank], writes=[st])
                    else:
                        kb.op("dve", lambda e, st=st, bank=bank: e.tensor_copy(out=st[:, :], in_=bank[:, :]), reads=[bank], writes=[st])
                    dst = rk_s.h.ap()[0 if mn == "r" else 1, ct, :, tsl]
                else:
                    st = stb[cnt["b"] % 3]
                    cnt["b"] += 1
                    if mn == "g":
                        kb.act(st[:, :], bank[:, :], AF.Silu, reads=[bank], writes=[st])
                        dst = sg_s.h.ap()[ct, :, tsl]
                    else:
                        kb.op("act", lambda e, st=st, bank=bank: e.copy(out=st[:, :], in_=bank[:, :]), reads=[bank], writes=[st])
                        dst = v_s.h.ap()[ct, :, tsl]
                kb.dma("sp", dst, st[:, :], reads=[st], key=st)
    for tb in range(NTB):
        tsl = slice(tb * 512, (tb + 1) * 512)
        X = load_x(tb)
        XM = xm[cnt["m"] % 2]
        cnt["m"] += 1
        make_mix(X, [("w", XM), ("a", xm2)])
        for (src, wl, dstT, fn) in ((XM, w1b, tanhT, AF.Tanh), (xm2, a1b, aT, AF.Copy)):
            bank = banks[cnt["e"] % 4]
            cnt["e"] += 1
            for k in range(KT):
                kb.mm(bank[0:96, :], wl[:, k, :], src[:, k, :], start=(k == 0), stop=(k == KT - 1),
                      reads=[wl, src], writes=[bank])
            kb.act(dstT[:, tsl], bank[0:96, :], fn, reads=[bank], writes=[dstT])
    kb.release(m0)
    if stop_after == "phase1":
        kb.finish()
        return kb

    Sf = [kb.sb(f"Sf{i}", [128, 128], F32) for i in range(8)]
    Sb = [kb.sb(f"Sb{i}", [128, 128], BF16) for i in range(8)]
    for i in range(8):
        kb.op("pool", lambda e, i=i: e.memset(Sf[i][:, :], 0.0), writes=[Sf[i]])
        kb.op("pool", lambda e, i=i: e.memset(Sb[i][:, :], 0.0), writes=[Sb[i]])
    AR = [[kb.sb(f"AR{j}_{i}", [128, NCK, 128], BF16) for i in range(8)] for j in range(2)]
    BK = [[kb.sb(f"BK{j}_{i}", [128, NCK, 128], BF16) for i in range(8)] for j in range(2)]
    BKH = [[kb.sb(f"BKH{j}_{i}", [128, NCK, 128], BF16) for i in range(8)] for j in range(2)]
    WC = [[kb.sb(f"WC{j}_{i}", [128, NCK], F32) for i in range(8)] for j in range(2)]
    BON = [[kb.sb(f"BON{j}_{i}", [128, BT], F32) for i in range(8)] for j in range(2)]
    VT = [[kb.sb(f"VT{j}_{i}", [128, 64 + BT], BF16) for i in range(8)] for j in range(2)]
    OAL = [[kb.sb(f"OAL{j}_{i}", [128, BT], F32) for i in range(8)] for j in range(2)]
    NW = 14
    wk_ = [kb.sb(f"wk{i}", [128, BT], F32) for i in range(NW)]
    wkb = [kb.sb(f"wkb{i}", [128, BT], BF16) for i in range(3)]
    rin = [kb.sb(f"rin{i}", [128, BT], F32) for i in range(2)]
    kin = [kb.sb(f"kin{i}", [128, BT], F32) for i in range(2)]
    sgin = [kb.sb(f"sgin{i}", [128, BT], BF16) for i in range(2)]
    osb = [kb.sb(f"osb{i}", [128, BT], BF16) for i in range(2)]
    Msb = [kb.sb(f"Msb{i}", [128, 4, 128], BF16) for i in range(2)]
    UVp = [kb.sb(f"UVp{i}", [128, 2, 2, 128], BF16) for i in range(2)]
    LBK = [kb.sb(f"LBK{i}", [128, 2, 2, 128], BF16) for i in range(2)]
    VZ = [kb.sb(f"VZ{i}", [128, 2, 2, 64], BF16) for i in range(2)]
    for i in range(2):
        kb.op("pool", lambda e, i=i: e.memset(VZ[i][:, :, :, :], 0.0), writes=[VZ[i]])
        kb.op("pool", lambda e, i=i: e.memset(UVp[i][:, :, :, :], 0.0), writes=[UVp[i]])
        kb.op("pool", lambda e, i=i: e.memset(LBK[i][:, :, :, :], 0.0), writes=[LBK[i]])
    Pm = [kb.sb(f"Pm{i}", [64, 4, 64], BF16) for i in range(2)]
    PTm = [kb.sb(f"PTm{i}", [64, 4, 64], BF16) for i in range(2)]
    Tm = [kb.sb(f"Tm{i}", [64, 4, 64], BF16) for i in range(2)]
    Xs = [kb.sb(f"Xs{i}", [64, 4, 64], BF16) for i in range(2)]
    pc = {"w": 0, "b": 0, "in": 0, "g": 0, "pb": 4}

    def W_():
        t = wk_[pc["w"] % NW]
        pc["w"] += 1
        return t

    def prep(tbk, j, ct):
        tsl = slice(tbk * BT, (tbk + 1) * BT)
        if True:
            csl = slice(ct * 128, (ct + 1) * 128)
            ii = pc["in"] % 2
            pc["in"] += 1
            R_, K_ = rin[ii], kin[ii]
            kb.dma("sp", R_[:, :], rk_s.h.ap()[0, ct, :, tsl], writes=[R_], key=R_)
            kb.dma("sp", K_[:, :], rk_s.h.ap()[1, ct, :, tsl], writes=[K_], key=K_)
            V_ = VT[j][ct]
            if tbk == 0:
                kb.op("pool", lambda e, V_=V_: e.memset(V_[:, 0:64], 0.0), writes=[V_])
                kb.dma("act", V_[:, 64:], v_s.h.ap()[ct, :, tsl], writes=[V_], key=V_)
            else:
                kb.dma("act", V_[:, :], v_s.h.ap()[ct, :, tbk * BT - 64:(tbk + 1) * BT], writes=[V_], key=V_)
            pbw = pbk[pc["pb"] % 2]; pc["pb"] += 1
            kb.mm(pbw[:, 0:BT], w2b[:, csl], tanhT[:, tsl], start=True, stop=True, reads=[w2b, tanhT], writes=[pbw])
            sg = W_()
            kb.act(sg[:, :], pbw[:, 0:BT], AF.Sigmoid, reads=[pbw, vec], writes=[sg], bias=vec[:, ct, 0:1])
            pba = pbk[pc["pb"] % 2]; pc["pb"] += 1
            kb.mm(pba[:, 0:BT], a2b[:, csl], aT[:, tsl], start=True, stop=True, reads=[a2b, aT], writes=[pba])
            av = W_()
            kb.act(av[:, :], pba[:, 0:BT], AF.Sigmoid, reads=[pba, vec], writes=[av], bias=vec[:, ct, 1:2])
            cs = W_()
            kb.op("dve", lambda e, cs=cs, sg=sg: e.tensor_tensor_scan(
                out=cs[:, :], data0=rmask[:, :], data1=sg[:, :], initial=0.0, op0=ALU.mult, op1=ALU.add),
                reads=[rmask, sg], writes=[cs])
            cm = W_()
            kb.op("pool", lambda e, cm=cm, cs=cs, sg=sg: e.tensor_tensor(out=cm[:, :], in0=cs[:, :], in1=sg[:, :], op=ALU.subtract),
                  reads=[cs, sg], writes=[cm])
            Ep = W_(); En = W_(); Epr = W_()
            kb.act(Ep[:, :], cs[:, :], AF.Exp, reads=[cs], writes=[Ep], scale=-DEC_C)
            kb.act(En[:, :], cs[:, :], AF.Exp, reads=[cs], writes=[En], scale=DEC_C)
            kb.act(Epr[:, :], cm[:, :], AF.Exp, reads=[cm], writes=[Epr], scale=-DEC_C)
            kb.op("pool", lambda e, j=j, ct=ct, Ep=Ep: e.tensor_copy(
                out=WC[j][ct][:, :], in_=sb_ap(Ep, 0, 128, 63, [[64, NCK]])), reads=[Ep], writes=[WC[j][ct]])
            kks = W_()
            kb.op("pool", lambda e, kks=kks, K_=K_, ct=ct: e.tensor_scalar(
                out=kks[:, :], in0=K_[:, :], scalar1=vec[:, ct, 2:3], scalar2=None, op0=ALU.mult), reads=[K_, vec], writes=[kks])
            sq = wkb[pc["b"] % 3]; pc["b"] += 1
            kb.act(sq[:, :], kks[:, :], AF.Square, reads=[kks], writes=[sq])
            pbs = pbk[pc["pb"] % 2]; pc["pb"] += 1
            kb.mm(pbs[:, 0:BT], BONES, sq[:, :], start=True, stop=True, reads=[cb, sq], writes=[pbs])
            rn = W_()
            kb.op("dve", lambda e, rn=rn, pbs=pbs: e.tensor_scalar(out=rn[:, :], in0=pbs[:, 0:BT], scalar1=1e-24, scalar2=None, op0=ALU.max),
                  reads=[pbs], writes=[rn])
            kb.op("act", lambda e, rn=rn: e.sqrt(out=rn[:, :], in_=rn[:, :]), reads=[rn], writes=[rn])
            kb.op("dve", lambda e, rn=rn: e.reciprocal(out=rn[:, :], in_=rn[:, :]), reads=[rn], writes=[rn])
            kkn = W_()
            kb.op("dve", lambda e, kkn=kkn, kks=kks, rn=rn: e.tensor_tensor(out=kkn[:, :], in0=kks[:, :], in1=rn[:, :], op=ALU.mult),
                  reads=[kks, rn], writes=[kkn])
            ARt, BKt, BKHt = AR[j][ct], BK[j][ct], BKH[j][ct]
            kb.op("dve", lambda e, ARt=ARt, kkn=kkn, Epr=Epr: e.scalar_tensor_tensor(
                out=ARt[:, :, 0:64], in0=kkn[:, :].rearrange("p (c t) -> p c t", t=64), scalar=-1.0,
                in1=Epr[:, :].rearrange("p (c t) -> p c t", t=64), op0=ALU.mult, op1=ALU.mult),
                reads=[kkn, Epr], writes=[ARt])
            kb.op("pool", lambda e, ARt=ARt, R_=R_, Ep=Ep: e.tensor_tensor(
                out=ARt[:, :, 64:128], in0=R_[:, :].rearrange("p (c t) -> p c t", t=64),
                in1=Ep[:, :].rearrange("p (c t) -> p c t", t=64), op=ALU.mult), reads=[R_, Ep], writes=[ARt])
            bt = W_()
            kb.op("dve", lambda e, bt=bt, kkn=kkn, av=av: e.tensor_tensor(out=bt[:, :], in0=kkn[:, :], in1=av[:, :], op=ALU.mult),
                  reads=[kkn, av], writes=[bt])
            btl = W_()
            kb.op("pool", lambda e, btl=btl, bt=bt, En=En: e.tensor_tensor(out=btl[:, :], in0=bt[:, :], in1=En[:, :], op=ALU.mult),
                  reads=[bt, En], writes=[btl])
            t1 = W_()
            kb.op("dve", lambda e, t1=t1, av=av, ct=ct: e.tensor_scalar(
                out=t1[:, :], in0=av[:, :], scalar1=-1.0, scalar2=vec[:, ct, 3:4], op0=ALU.add, op1=ALU.mult),
                reads=[av, vec], writes=[t1])
            kp = W_()
            kb.op("dve", lambda e, kp=kp, t1=t1, K_=K_: e.scalar_tensor_tensor(
                out=kp[:, :], in0=t1[:, :], scalar=1.0, in1=K_[:, :], op0=ALU.add, op1=ALU.mult), reads=[t1, K_], writes=[kp])
            ktl = W_()
            kb.op("pool", lambda e, ktl=ktl, kp=kp, En=En: e.tensor_tensor(out=ktl[:, :], in0=kp[:, :], in1=En[:, :], op=ALU.mult),
                  reads=[kp, En], writes=[ktl])
            for (src, off) in ((btl, 0), (ktl, 64)):
                kb.op("act", lambda e, BKt=BKt, src=src, off=off: e.copy(
                    out=BKt[:, :, off:off + 64], in_=src[:, :].rearrange("p (c t) -> p c t", t=64)), reads=[src], writes=[BKt])
                kb.op("dve", lambda e, BKHt=BKHt, src=src, off=off, Ep=Ep: e.tensor_tensor(
                    out=BKHt[:, :, off:off + 64], in0=src[:, :].rearrange("p (c t) -> p c t", t=64),
                    in1=sb_ap(Ep, 0, 128, 63, [[64, NCK], [0, 64]]), op=ALU.mult), reads=[src, Ep], writes=[BKHt])
            pp = wkb[pc["b"] % 3]; pc["b"] += 1
            kb.op("dve", lambda e, pp=pp, R_=R_, kp=kp, ct=ct: e.scalar_tensor_tensor(
                out=pp[:, :], in0=R_[:, :], scalar=vec[:, ct, 4:5], in1=kp[:, :], op0=ALU.mult, op1=ALU.mult),
                reads=[R_, kp, vec], writes=[pp])
            pbb = pbk[pc["pb"] % 2]; pc["pb"] += 1
            kb.mm(pbb[:, 0:BT], BONES, pp[:, :], start=True, stop=True, reads=[cb, pp], writes=[pbb])
            kb.op("dve", lambda e, j=j, ct=ct, pbb=pbb, V_=V_: e.tensor_tensor(
                out=BON[j][ct][:, :], in0=pbb[:, 0:BT], in1=V_[:, 64:], op=ALU.mult), reads=[pbb, V_], writes=[BON[j][ct]])

    def scan_step(tbk, j, c, gq):
        s = pc["g"] % 2
        pc["g"] += 1
        M_, UV_, LB_, P_, PT_, T_, X_, VZ_ = Msb[s], UVp[s], LBK[s], Pm[s], PTm[s], Tm[s], Xs[s], VZ[s]
        cts = (2 * gq, 2 * gq + 1)
        heads = [(pi, hd) for hd in range(2) for pi in range(2)]
        for hi, (pi, hd) in enumerate(heads):
            ct = cts[pi]
            ps_ = slice(hd * 64, hd * 64 + 64)
            kb.mm(bM[hd][:, pi * 128:(pi + 1) * 128], BK[j][ct][ps_, c, :], AR[j][ct][ps_, c, :], start=True, stop=True,
                  reads=[BK[j][ct], AR[j][ct]], writes=[bM[hd]])
        if upto <= 0.5:
            return
        for hd in range(2):
            kb.op("dve", lambda e, hd=hd: e.tensor_tensor(
                out=M_[:, 2 * hd:2 * hd + 2, :], in0=bM[hd][:, 0:256].rearrange("p (h t) -> p h t", t=128),
                in1=sb_ap(cfm, 0, 128, 0, [[0, 2], [1, 128]]), op=ALU.mult), reads=[bM[hd], cfm], writes=[M_])
        if upto <= 1:
            return
        for hi, (pi, hd) in enumerate(heads):
            ct = cts[pi]
            ps_ = slice(hd * 64, hd * 64 + 64)
            kb.mm(bM[hd][0:64, 256 + pi * 64:256 + (pi + 1) * 64], AR[j][ct][ps_, c, 0:64], BK[j][ct][ps_, c, 0:64],
                  start=True, stop=True, reads=[BK[j][ct], AR[j][ct]], writes=[bM[hd]])
        for hd in range(2):
            kb.op("dve", lambda e, hd=hd: e.tensor_tensor(
                out=PT_[:, 2 * hd:2 * hd + 2, :], in0=bM[hd][0:64, 256:384].rearrange("p (h t) -> p h t", t=64),
                in1=sb_ap(cfm, 0, 64, 128, [[0, 2], [1, 64]]), op=ALU.mult), reads=[bM[hd], cfm], writes=[PT_])
        if upto <= 2:
            return
        kb.op("pool", lambda e: e.tensor_tensor(
            out=T_[:, :, :], in0=M_[0:64, :, 0:64], in1=sb_ap(cfm, 0, 64, 192, [[0, 4], [1, 64]]), op=ALU.add),
            reads=[M_, cfm], writes=[T_])
        if upto <= 3:
            return
        Pcur = None
        for lv in range(1, 6):
            last = (lv == 5)
            for hi in range(4):
                Pk = M_[0:64, hi, 0:64] if Pcur is None else P_[:, hi, :]
                PTk = PT_[:, hi, :]
                if not last:
                    kb.mm(bP[0:64, hi * 64:(hi + 1) * 64], PTk, Pk, start=True, stop=True,
                          reads=[PT_, M_ if Pcur is None else P_], writes=[bP])
                kb.mm(bT[0:64, hi * 64:(hi + 1) * 64], Pk, PTk, start=True, stop=True,
                      reads=[PT_, M_ if Pcur is None else P_], writes=[bT])
            if upto <= 3.1:
                return
            kb.op("act", lambda e: e.copy(out=PT_[:, :, :], in_=bT[0:64, 0:256].rearrange("p (h t) -> p h t", t=64)),
                  reads=[bT], writes=[PT_])
            if upto <= 3.2:
                return
            if not last:
                kb.op("dve", lambda e: e.tensor_copy(out=P_[:, :, :], in_=bP[0:64, 0:256].rearrange("p (h t) -> p h t", t=64)),
                      reads=[bP], writes=[P_])
                Pcur = P_
            if upto <= 3.3:
                return
            for hi in range(4):
                kb.mm(bTU[0:64, hi * 64:(hi + 1) * 64], PT_[:, hi, :], T_[:, hi, :], start=True, stop=True,
                      reads=[PT_, T_], writes=[bTU])
            if upto <= 3.4:
                return
            kb.op("dve", lambda e: e.tensor_tensor(
                out=T_[:, :, :], in0=bTU[0:64, 0:256].rearrange("p (h t) -> p h t", t=64), in1=T_[:, :, :], op=ALU.add),
                reads=[bTU, T_], writes=[T_])
        if upto <= 4:
            return
        for pi in range(2):
            ct = cts[pi]
            kb.op("pe", lambda e, pi=pi, ct=ct: e.transpose(
                bTRb[:, pi * 128:(pi + 1) * 128], VT[j][ct][:, c * 64:c * 64 + 128], IDENT), reads=[VT[j][ct], cb], writes=[bTRb])
            kb.op("pe", lambda e, pi=pi, ct=ct: e.transpose(
                bTRb[:, 256 + pi * 128:256 + (pi + 1) * 128], BKH[j][ct][:, c, :], IDENT), reads=[BKH[j][ct], cb], writes=[bTRb])
        if upto <= 5:
            return
        for hd in range(2):
            cs_ = slice(hd * 64, hd * 64 + 64)
            kb.op("act", lambda e, hd=hd, cs_=cs_: e.copy(
                out=UV_[64:128, :, hd, cs_], in_=bTRb[64:128, 0:256].rearrange("p (q t) -> p q t", t=128)[:, :, cs_]),
                reads=[bTRb], writes=[UV_])
            kb.op("pool" if False else "act", lambda e, hd=hd, cs_=cs_: e.copy(
                out=VZ_[64:128, :, hd, :], in_=bTRb[64:128, 0:256].rearrange("p (q t) -> p q t", t=128)[:, :, cs_]),
                reads=[bTRb], writes=[VZ_])
            kb.op("act", lambda e, hd=hd, cs_=cs_: e.copy(
                out=LB_[:, :, hd, cs_], in_=bTRb[:, 256:512].rearrange("p (q t) -> p q t", t=128)[:, :, cs_]),
                reads=[bTRb], writes=[LB_])
        if upto <= 6:
            return
        for pi in range(2):
            ct = cts[pi]
            kb.mm(bX[0:64, pi * 128:(pi + 1) * 128], AR[j][ct][:, c, 0:64], Sb[ct][:, :], start=True, stop=False,
                  reads=[AR[j][ct], Sb[ct]], writes=[bX])
            for hd in range(2):
                kb.mm(bX[0:64, pi * 128 + hd * 64:pi * 128 + (hd + 1) * 64], M_[:, 2 * hd + pi, 0:64], VZ_[:, pi, hd, :],
                      start=False, stop=(hd == 1), reads=[M_, VZ_], writes=[bX])
        if upto <= 7:
            return
        kb.op("act", lambda e: e.copy(out=X_[:, :, :], in_=bX[0:64, 0:256].rearrange("p (h t) -> p h t", t=64)),
              reads=[bX], writes=[X_])
        for pi in range(2):
            for hd in range(2):
                kb.mm(bU[0:64, (2 * pi + hd) * 64:(2 * pi + hd + 1) * 64], T_[:, 2 * hd + pi, :], X_[:, 2 * pi + hd, :],
                      start=True, stop=True, reads=[T_, X_], writes=[bU])
        for hd in range(2):
            cs_ = slice(hd * 64, hd * 64 + 64)
            kb.op("dve", lambda e, hd=hd, cs_=cs_: e.tensor_copy(
                out=UV_[0:64, :, hd, cs_], in_=bU[0:64, 0:256].rearrange("p (q h t) -> p q h t", h=2, t=64)[:, :, hd, :]),
                reads=[bU], writes=[UV_])
        if upto <= 8:
            return
        for pi in range(2):
            ct = cts[pi]
            oreg = bOS[:, pi * 64:(pi + 1) * 64]
            kb.mm(oreg, Sb[ct][:, :], AR[j][ct][:, c, 64:128], start=True, stop=False, reads=[Sb[ct], AR[j][ct]], writes=[bOS])
            for hd in range(2):
                kb.mm(oreg, UV_[:, pi, hd, :], M_[:, 2 * hd + pi, 64:128], start=False, stop=(hd == 1),
                      reads=[UV_, M_], writes=[bOS])
            kb.op("dve", lambda e, pi=pi, ct=ct: e.tensor_copy(out=OAL[j][ct][:, c * 64:(c + 1) * 64], in_=bOS[:, pi * 64:(pi + 1) * 64]),
                  reads=[bOS], writes=[OAL[j][ct]])
            dreg = bOS[:, 128 + pi * 128:128 + (pi + 1) * 128]
            for hd in range(2):
                kb.mm(dreg, LB_[:, pi, hd, :], UV_[:, pi, hd, :], start=(hd == 0), stop=(hd == 1),
                      reads=[LB_, UV_], writes=[bOS])
            kb.op("dve", lambda e, pi=pi, ct=ct: e.scalar_tensor_tensor(
                out=Sf[ct][:, :], in0=Sf[ct][:, :], scalar=WC[j][ct][:, c:c + 1], in1=bOS[:, 128 + pi * 128:128 + (pi + 1) * 128],
                op0=ALU.mult, op1=ALU.add), reads=[Sf[ct], WC[j][ct], bOS], writes=[Sf[ct]])
            kb.op("act", lambda e, ct=ct: e.copy(out=Sb[ct][:, :], in_=Sf[ct][:, :]), reads=[Sf[ct]], writes=[Sb[ct]])

    def outstage(tbk, j):
        tsl = slice(tbk * BT, (tbk + 1) * BT)
        for ct in range(8):
            O_ = OAL[j][ct]
            ii = pc["in"] % 2
            pc["in"] += 1
            G_ = sgin[ii]
            kb.dma("act", G_[:, :], sg_s.h.ap()[ct, :, tsl], writes=[G_], key=G_)
            ob = wkb[pc["b"] % 3]; pc["b"] += 1
            kb.op("act", lambda e, ob=ob, O_=O_: e.copy(out=ob[:, :], in_=O_[:, :]), reads=[O_], writes=[ob])
            o2 = wkb[pc["b"] % 3]; pc["b"] += 1
            kb.act(o2[:, :], O_[:, :], AF.Square, reads=[O_], writes=[o2])
            pm = pbk[pc["pb"] % 2]; pc["pb"] += 1
            kb.mm(pm[:, 0:BT], BONES, ob[:, :], start=True, stop=True, reads=[cb, ob], writes=[pm])
            mean = W_()
            kb.op("dve", lambda e, mean=mean, pm=pm: e.tensor_scalar(out=mean[:, :], in0=pm[:, 0:BT], scalar1=1.0 / 64, scalar2=None, op0=ALU.mult),
                  reads=[pm], writes=[mean])
            p2 = pbk[pc["pb"] % 2]; pc["pb"] += 1
            kb.mm(p2[:, 0:BT], BONES, o2[:, :], start=True, stop=True, reads=[cb, o2], writes=[p2])
            var = W_()
            kb.op("dve", lambda e, var=var, mean=mean: e.tensor_tensor(out=var[:, :], in0=mean[:, :], in1=mean[:, :], op=ALU.mult),
                  reads=[mean], writes=[var])
            kb.op("dve", lambda e, var=var, p2=p2: e.scalar_tensor_tensor(
                out=var[:, :], in0=p2[:, 0:BT], scalar=1.0 / 64, in1=var[:, :], op0=ALU.mult, op1=ALU.subtract),
                reads=[p2, var], writes=[var])
            kb.op("dve", lambda e, var=var: e.tensor_scalar(out=var[:, :], in0=var[:, :], scalar1=0.0, scalar2=GN_EPS, op0=ALU.max, op1=ALU.add),
                  reads=[var], writes=[var])
            kb.op("act", lambda e, var=var: e.sqrt(out=var[:, :], in_=var[:, :]), reads=[var], writes=[var])
            kb.op("dve", lambda e, var=var: e.reciprocal(out=var[:, :], in_=var[:, :]), reads=[var], writes=[var])
            y = W_()
            kb.op("pool", lambda e, y=y, O_=O_, mean=mean: e.tensor_tensor(out=y[:, :], in0=O_[:, :], in1=mean[:, :], op=ALU.subtract),
                  reads=[O_, mean], writes=[y])
            kb.op("dve", lambda e, y=y, var=var, ct=ct: e.scalar_tensor_tensor(
                out=y[:, :], in0=y[:, :], scalar=vec[:, ct, 5:6], in1=var[:, :], op0=ALU.mult, op1=ALU.mult),
                reads=[y, var, vec], writes=[y])
            kb.op("dve", lambda e, y=y, ct=ct, j=j: e.scalar_tensor_tensor(
                out=y[:, :], in0=y[:, :], scalar=vec[:, ct, 6:7], in1=BON[j][ct][:, :], op0=ALU.add, op1=ALU.add),
                reads=[y, vec, BON[j][ct]], writes=[y])
            os_ = osb[ii]
            kb.op("pool", lambda e, os_=os_, y=y, G_=G_: e.tensor_tensor(out=os_[:, :], in0=y[:, :], in1=G_[:, :], op=ALU.mult),
                  reads=[y, G_], writes=[os_])
            kb.dma("sp", oT.h.ap()[ct * 128:(ct + 1) * 128, tsl], os_[:, :], reads=[os_], key=os_)

    for ct in range(8):
        prep(0, 0, ct)
    if stop_after == "prep":
        kb.barrier()
        kb.finish()
        return kb
    if stop_after == "scan1":
        scan_step(0, 0, 0, 0)
        kb.barrier()
        kb.finish()
        return kb
    per = 8 // NCK
    for tbk in range(NB2):
        j = tbk % 2
        for c in range(NCK):
            for gq in range(4):
                scan_step(tbk, j, c, gq)
            if tbk + 1 < NB2:
                for ct in range(c * per, (c + 1) * per):
                    prep(tbk + 1, 1 - j, ct)
        outstage(tbk, j)
    kb.barrier()
    kb.finish()
    return kb


def host_consts_c(BT=256):
    import ml_dtypes
    ident = np.eye(128, dtype=np.float32)
    bones = np.zeros((128, 128), np.float32)
    bones[:64, :64] = 1.0
    bones[64:, 64:] = 1.0
    ccb = np.concatenate([ident, bones], axis=1).astype(ml_dtypes.bfloat16)
    r = np.arange(128)[:, None] % 64
    q = np.arange(128)[None, :]
    t = q % 64
    maskall = np.where(q < 64, r < t, r <= t).astype(np.float32)
    ccf = np.zeros((128, 320), np.float32)
    ccf[:, 0:128] = maskall
    tt = np.arange(64)[:, None]
    jj = np.arange(64)[None, :]
    ccf[:64, 128:192] = (jj < tt).astype(np.float32)
    ccf[:64, 192:256] = np.eye(64, dtype=np.float32)
    rmk = np.ones((128, BT), np.float32)
    rmk[:, 0::64] = 0.0
    return ccb, ccf, rmk


def prep_rwkv(x1T_bf, P, hh, BT=256):
    sl = slice(hh * 1024, (hh + 1) * 1024)
    c = lambda a: np.ascontiguousarray(a)
    vecs = np.zeros((128, 8, 8), np.float32)
    for i, nm in enumerate(("w0", "a0", "k_k", "k_a", "r_k", "gn_w", "gn_b")):
        v = np.asarray(P[nm]).reshape(-1)[sl]
        vecs[:, :, i] = v.reshape(8, 128).T
    mu6 = np.ascontiguousarray(np.asarray(P["mu"]).reshape(6, 16, 128).transpose(2, 0, 1))
    ccb, ccf, rmk = host_consts_c(BT)
    return dict(x1T=x1T_bf, wr=c(P["w_r"][:, sl]), wk=c(P["w_k"][:, sl]), wv=c(P["w_v"][:, sl]), wg=c(P["w_g"][:, sl]),
                w1=c(P["w1"]), a1=c(P["a1"]), w2=c(P["w2"][:, sl]), a2=c(P["a2"][:, sl]),
                mu6=mu6, vecs=vecs, ccb=ccb, ccf=ccf, rmk=rmk)


_CACHE = {}


def _prog(name, fn):
    if name not in _CACHE:
        _CACHE[name] = fn()
    return _CACHE[name]


def kernel(x, ev_w_in, ev_w_pool, ev_pool_scale, ev_w_out,
           od_mu, od_w_r, od_w_k, od_w_v, od_w_g, od_w0, od_w1, od_w2,
           od_a0, od_a1, od_a2, od_k_k, od_k_a, od_r_k, od_gn_w, od_gn_b, od_w_o,
           ln_g, ln_b):
    from concourse.bass_utils import run_bass_kernel_spmd
    f = lambda a: np.asarray(a, dtype=np.float32)
    x = f(x)
    B, T, _ = x.shape
    H2 = T // 2
    cores = list(range(8))
    kbA = _prog("A", lambda: build_mixer0(T))
    w_in, w_pool, pscale = f(ev_w_in)[0], f(ev_w_pool)[0], f(ev_pool_scale)[0]
    insA = [prep_mixer0(x[c // 2], w_in, w_pool, pscale, c % 2) for c in cores]
    rA = run_bass_kernel_spmd(kbA.nc, insA, core_ids=cores).results
    mT = []
    for b in range(B):
        c0, c1 = rA[2 * b]["mixT"], rA[2 * b + 1]["mixT"]
        mT.append(np.concatenate([c0[0:512], c1[0:512], c0[512:1024], c1[512:1024]], axis=0))
    kbB = _prog("B", lambda: build_outproj(H2, True))
    g0, b0 = f(ln_g)[0:1], f(ln_b)[0:1]
    w_out = f(ev_w_out)[0]
    insB = [dict(mT=np.ascontiguousarray(mT[c // 2][:, (c % 2) * H2:(c % 2 + 1) * H2]), w=w_out,
                 x=np.ascontiguousarray(x[c // 2, (c % 2) * H2:(c % 2 + 1) * H2]), g=g0, b=b0) for c in cores]
    rB = run_bass_kernel_spmd(kbB.nc, insB, core_ids=cores).results
    x1 = [np.concatenate([rB[2 * b]["y"], rB[2 * b + 1]["y"]], axis=0) for b in range(B)]
    x1T = [np.ascontiguousarray(np.concatenate([rB[2 * b]["yb"], rB[2 * b + 1]["yb"]], axis=0).T) for b in range(B)]
    kbC = _prog("C", lambda: build_rwkv(T))
    P = dict(mu=f(od_mu)[0], w_r=f(od_w_r)[0], w_k=f(od_w_k)[0], w_v=f(od_w_v)[0], w_g=f(od_w_g)[0],
             w0=f(od_w0)[0], w1=f(od_w1)[0], w2=f(od_w2)[0], a0=f(od_a0)[0], a1=f(od_a1)[0], a2=f(od_a2)[0],
             k_k=f(od_k_k)[0], k_a=f(od_k_a)[0], r_k=f(od_r_k)[0], gn_w=f(od_gn_w)[0], gn_b=f(od_gn_b)[0])
    insC = [prep_rwkv(x1T[c // 2], P, c % 2) for c in cores]
    rC = run_bass_kernel_spmd(kbC.nc, insC, core_ids=cores).results
    oT = [np.concatenate([rC[2 * b]["oT"], rC[2 * b + 1]["oT"]], axis=0) for b in range(B)]
    g1, b1 = f(ln_g)[1:2], f(ln_b)[1:2]
    w_o = f(od_w_o)[0]
    insD = [dict(mT=np.ascontiguousarray(oT[c // 2][:, (c % 2) * H2:(c % 2 + 1) * H2]), w=w_o,
                 x=np.ascontiguousarray(x1[c // 2][(c % 2) * H2:(c % 2 + 1) * H2]), g=g1, b=b1) for c in cores]
    rD = run_bass_kernel_spmd(kbB.nc, insD, core_ids=cores).results
    out = np.stack([np.concatenate([rD[2 * b]["y"], rD[2 * b + 1]["y"]], axis=0) for b in range(B)], axis=0)
    return out.astype(np.float32)
```
